# Optimizing a Trainium2 kernel written in Bass

```python
import math
import jax, jax.numpy as jnp
from jax import lax
import numpy as np

D_MODEL = 2048
BATCH = 4
SEQ = 8192
DEPTH = 4

D_MIX = D_MODEL
ATTN_WIDTH = D_MIX // 2
POOL_WIDTH = D_MIX - ATTN_WIDTH
N_HEADS = 8
D_V = ATTN_WIDTH // N_HEADS
D_QK = D_V // 2
Q_COLS = N_HEADS * 2 * D_QK
POOL_WINDOWS = (2, 4, 8, 16)
N_POOL_GROUPS = len(POOL_WINDOWS)
POOL_GROUP = POOL_WIDTH // N_POOL_GROUPS
IN_COLS = 2 * Q_COLS + ATTN_WIDTH + POOL_WIDTH
D_FF = 5632
CONV_WIDTH = 3
Q_BLOCK = 128
EPS = 1e-6
LAMBDA_STD = 0.1

kernel_name = "hybrid_diffattn_multipool_sandwich"


def rms_norm(x, g):
    xf = x.astype(jnp.float32)
    y = xf * lax.rsqrt(jnp.mean(xf * xf, axis=-1, keepdims=True) + EPS)
    return (y * g.astype(jnp.float32)).astype(x.dtype)


def alibi_slopes(n_heads):
    return 2.0 ** (-8.0 * jnp.arange(1, n_heads + 1, dtype=jnp.float32) / n_heads)


def diff_attention(q, k, v, lam):
    B, S, H, _, Dk = q.shape
    nb = S // Q_BLOCK
    scale = Dk ** -0.5
    slopes = alibi_slopes(H)
    kpos = jnp.arange(S)
    q_blocks = q.reshape(B, nb, Q_BLOCK, H, 2, Dk).transpose(1, 0, 2, 3, 4, 5)

    def one_block(args):
        q_blk, blk = args
        qpos = blk * Q_BLOCK + jnp.arange(Q_BLOCK)
        dist = qpos[:, None] - kpos[None, :]
        bias = -slopes[:, None, None] * dist.astype(jnp.float32)[None]
        bias = jnp.where((dist >= 0)[None], bias, -jnp.inf)
        logits = jnp.einsum('bqhmd,bkhmd->bhmqk', q_blk, k).astype(jnp.float32) * scale
        a = jax.nn.softmax(logits + bias[None, :, None], axis=-1)
        p = a[:, :, 0] - lam * a[:, :, 1]
        return jnp.einsum('bhqk,bkhe->bqhe', p.astype(v.dtype), v)

    o = lax.map(one_block, (q_blocks, jnp.arange(nb)))
    return o.transpose(1, 0, 2, 3, 4).reshape(B, S, H, v.shape[-1])


def multiscale_pool(u):
    B, S, _ = u.shape
    ug = u.reshape(B, S, N_POOL_GROUPS, POOL_GROUP).astype(jnp.float32)
    cs = jnp.cumsum(ug, axis=1)
    t = jnp.arange(1, S + 1, dtype=jnp.float32)
    outs = []
    for g, w in enumerate(POOL_WINDOWS):
        c = cs[:, :, g]
        prev = jnp.pad(c, ((0, 0), (w, 0), (0, 0)))[:, :S]
        cnt = jnp.minimum(t, float(w))[None, :, None]
        outs.append((c - prev) / cnt - ug[:, :, g])
    return jnp.stack(outs, axis=2)


def causal_dwconv(h, w, b):
    K = w.shape[0]
    S = h.shape[1]
    hp = jnp.pad(h, ((0, 0), (K - 1, 0), (0, 0)))
    y = b
    for j in range(K):
        y = y + hp[:, j:j + S] * w[j]
    return y


def setup_inputs(seed: int = 0) -> dict:
    key = jax.random.key(seed)
    ks = jax.random.split(key, 20)
    f32 = jnp.float32

    def nrm(k, shape, scale):
        return jax.random.normal(k, shape, f32) * scale

    def gain(k, n):
        return 1.0 + 0.02 * jax.random.normal(k, (DEPTH, n), f32)

    return {
        "x": jax.random.normal(ks[0], (BATCH, SEQ, D_MODEL), f32),
        "g_mix_pre": gain(ks[1], D_MODEL),
        "w_in": nrm(ks[2], (DEPTH, D_MODEL, IN_COLS), D_MODEL ** -0.5),
        "lam_q1": nrm(ks[3], (DEPTH, D_QK), LAMBDA_STD),
        "lam_k1": nrm(ks[4], (DEPTH, D_QK), LAMBDA_STD),
        "lam_q2": nrm(ks[5], (DEPTH, D_QK), LAMBDA_STD),
        "lam_k2": nrm(ks[6], (DEPTH, D_QK), LAMBDA_STD),
        "g_head": gain(ks[7], ATTN_WIDTH),
        "w_pool": nrm(ks[8], (DEPTH, N_POOL_GROUPS, POOL_GROUP, POOL_GROUP), POOL_GROUP ** -0.5),
        "pool_scale": gain(ks[9], POOL_WIDTH),
        "w_out": nrm(ks[10], (DEPTH, D_MIX, D_MODEL), D_MIX ** -0.5),
        "g_mix_post": gain(ks[11], D_MODEL),
        "g_ffn_pre": gain(ks[12], D_MODEL),
        "w_up": nrm(ks[13], (DEPTH, D_MODEL, 2 * D_FF), D_MODEL ** -0.5),
        "conv_w": nrm(ks[14], (DEPTH, CONV_WIDTH, 2 * D_FF), CONV_WIDTH ** -0.5),
        "conv_b": nrm(ks[15], (DEPTH, 2 * D_FF), 0.01),
        "w_down": nrm(ks[16], (DEPTH, D_FF, D_MODEL), D_FF ** -0.5),
        "g_ffn_post": gain(ks[17], D_MODEL),
    }


def reference(x, g_mix_pre, w_in, lam_q1, lam_k1, lam_q2, lam_k2, g_head, w_pool,
              pool_scale, w_out, g_mix_post, g_ffn_pre, w_up, conv_w, conv_b, w_down,
              g_ffn_post):
    B, S, _ = x.shape
    for i in range(DEPTH):
        h = rms_norm(x, g_mix_pre[i])
        proj = h @ w_in[i]
        q = proj[..., :Q_COLS].reshape(B, S, N_HEADS, 2, D_QK)
        k = proj[..., Q_COLS:2 * Q_COLS].reshape(B, S, N_HEADS, 2, D_QK)
        v = proj[..., 2 * Q_COLS:2 * Q_COLS + ATTN_WIDTH].reshape(B, S, N_HEADS, D_V)
        u = proj[..., 2 * Q_COLS + ATTN_WIDTH:]

        lam_init = 0.8 - 0.6 * math.exp(-0.3 * i)
        lam = (jnp.exp(jnp.sum(lam_q1[i].astype(jnp.float32) * lam_k1[i].astype(jnp.float32)))
               - jnp.exp(jnp.sum(lam_q2[i].astype(jnp.float32) * lam_k2[i].astype(jnp.float32)))
               + lam_init)
        o = diff_attention(q, k, v, lam)
        o = rms_norm(o, g_head[i].reshape(N_HEADS, D_V)) * (1.0 - lam_init)
        o = o.reshape(B, S, ATTN_WIDTH)

        pm = multiscale_pool(u).astype(x.dtype)
        pm = jnp.einsum('bsgc,gcd->bsgd', pm, w_pool[i]).reshape(B, S, POOL_WIDTH) * pool_scale[i]

        mix = jnp.concatenate([o, pm], axis=-1) @ w_out[i]
        x = x + rms_norm(mix, g_mix_post[i])

        h = rms_norm(x, g_ffn_pre[i])
        up = causal_dwconv(h @ w_up[i], conv_w[i], conv_b[i])
        gate, val = up[..., :D_FF], up[..., D_FF:]
        y = (jax.nn.gelu(gate, approximate=True) * val) @ w_down[i]
        x = x + rms_norm(y, g_ffn_post[i])
    return x
```

```python
import math
from contextlib import ExitStack

import numpy as np
import ml_dtypes

import concourse.bass as bass
import concourse.mybir as mybir
from concourse.bass_utils import run_bass_kernel_spmd

F32, BF16 = mybir.dt.float32, mybir.dt.bfloat16
AF = mybir.ActivationFunctionType
ALU = mybir.AluOpType
AX = mybir.AxisListType
NPBF = ml_dtypes.bfloat16

D = 2048
KC = 16
NH = 8
DFF = 5632
FC = 44
DEPTH = 4
EPS = 1e-6
THR = 40.0
NEG = -30000.0
SLOPES = [2.0 ** (-(h + 1)) for h in range(NH)]
ENGS = ("pe", "act", "dve", "pool", "sp")


class Prog:
    NPOOL = 56

    def __init__(self, nc, stack):
        self.nc, self.stack = nc, stack
        self.sems, self.val = {}, {}
        self.free = [stack.enter_context(nc.semaphore("sem%d" % i)) for i in range(self.NPOOL)]

    def sem(self, name):
        if name not in self.sems:
            self.sems[name] = self.free.pop(0)
            self.val[name] = 0
        return self.sems[name]


class Phase:
    def __init__(self, prog):
        self.p = prog
        prog.nphase = getattr(prog, "nphase", 0) + 1
        self.tag = prog.nphase % 2
        self.es = {e: "%s@%d" % (e, self.tag) for e in ENGS}
        for e in ENGS:
            if self.es[e] in prog.val:
                prog.val[self.es[e]] = 0
        self.ops = {e: [] for e in ENGS}
        self.lw, self.rd = {}, {}
        self.waited = {e: {} for e in ENGS}

    def _deps(self, eng, reads, writes):
        need = {}

        def req(t):
            if t is not None and need.get(t[0], 0) < t[1]:
                need[t[0]] = t[1]

        for k in reads:
            req(self.lw.get(k))
        for k in writes:
            req(self.lw.get(k))
            for t in self.rd.get(k, ()):
                req(t)
        waits = []
        for s, v in need.items():
            if eng == "pe" and s == self.es["pe"]:
                continue
            if self.waited[eng].get(s, 0) >= v:
                continue
            self.waited[eng][s] = v
            waits.append((s, v))
        return waits

    def _commit(self, t, reads, writes):
        for k in reads:
            self.rd.setdefault(k, []).append(t)
        for k in writes:
            self.lw[k] = t
            self.rd[k] = []

    def op(self, eng, fn, reads=(), writes=()):
        waits = self._deps(eng, reads, writes)
        sn = self.es[eng]
        self.p.sem(sn)
        self.p.val[sn] += 1
        t = (sn, self.p.val[sn])
        self.ops[eng].append((waits, fn, t))
        self._commit(t, reads, writes)

    def dma(self, q, fn, n, sem, reads=(), writes=(), inc=16):
        waits = self._deps(q, reads, writes)
        self.p.sem(sem)
        self.p.val[sem] += inc * n
        t = (sem, self.p.val[sem])
        self.ops[q].append((waits, fn, ("dma", sem, inc)))
        self._commit(t, reads, writes)

    def emit(self):
        nc, P = self.p.nc, self.p
        finals = [(s, v) for s, v in P.val.items() if "@" not in s and v > 0]
        with nc.Block() as block:
            def mk(e):
                def body(eng):
                    other = "%s@%d" % (e, 1 - self.tag)
                    if other in P.sems:
                        eng.sem_clear(P.sems[other])
                    for waits, fn, t in self.ops[e]:
                        for s, v in waits:
                            eng.wait_ge(P.sems[s], v)
                        if t[0] == "dma":
                            sem = P.sems[t[1]]
                            fn(eng, lambda ins, sem=sem, iv=t[2]: ins.then_inc(sem, iv))
                        else:
                            fn(eng).then_inc(P.sems[t[0]], 1)
                    if e == "sp":
                        for s, v in finals:
                            eng.wait_ge(P.sems[s], v)
                return body

            block.tensor(mk("pe"))
            block.scalar(mk("act"))
            block.vector(mk("dve"))
            block.gpsimd(mk("pool"))
            block.sync(mk("sp"))


class Rot:
    def __init__(self, items):
        self.items, self.i = items, 0

    def next(self):
        it = self.items[self.i % len(self.items)]
        self.i += 1
        return it


def tiles_of(NT):
    out = [(0, 1)]
    for j in range(NT):
        out.append((128 + 512 * j, 4))
    return out


def alibi_cols(NT):
    NBH = 4 * NT
    cols, vals = {}, []
    for h in range(NH):
        back = int((THR / SLOPES[h] - 1) // 128) + 1
        for nsub in (1, 4):
            for r in range(-min(back, 2 * NBH), nsub):
                cols[(h, nsub, r)] = len(vals)
                ki = np.arange(128, dtype=np.float64)
                vals.append(SLOPES[h] * (ki + 128.0 * r - 64.0 * nsub))
    return cols, np.stack(vals, axis=1).astype(np.float32)


def const_tables(NT):
    ident = np.eye(128, dtype=np.float32).astype(NPBF)
    ki = np.arange(128)[:, None, None]
    r = np.arange(4)[None, :, None]
    qi = np.arange(512)[None, None, :]
    mask = np.where(ki - qi <= -128 * r, 0.0, NEG).astype(np.float32).astype(NPBF)
    cols, alibi = alibi_cols(NT)
    return ident, mask, cols, alibi


def rms_rstd(ph, ss, rstd, n, keyss, keyr, ncols, epsb):
    ph.op("act", lambda e: e.activation(out=ss[:, 0:ncols], in_=ss[:, 0:ncols], func=AF.Sqrt,
                                        bias=epsb[:, 0:1], scale=1.0 / n),
          reads=[keyss], writes=[keyss])
    ph.op("dve", lambda e: e.reciprocal(out=rstd[:, 0:ncols], in_=ss[:, 0:ncols]),
          reads=[keyss], writes=[keyr])


def norm_transpose(ph, T, xt, xkey, nsub, gT, gkey, hT, hkey, pfx):
    ss, rstd, junk, hn = T["ss"], T["rstd"], T["junk"], T["hn"]
    hnk = T.get("hnkeys", [("hn", s) for s in range(4)])
    for s in range(nsub):
        ph.op("act", lambda e, s=s: e.activation(out=junk[:], in_=xt[:, s, :], func=AF.Square,
                                                 accum_out=ss[:, s:s + 1]),
              reads=[xkey], writes=["junk", "ss"])
    rms_rstd(ph, ss, rstd, D, "ss", "rstd", nsub, T["epsb"])
    for s in range(nsub):
        ph.op("dve", lambda e, s=s: e.tensor_scalar(out=hn[:, s, :], in0=xt[:, s, :],
                                                    scalar1=rstd[:, s:s + 1], scalar2=None,
                                                    op0=ALU.mult),
              reads=[xkey, "rstd"], writes=[hnk[s]])
    for kc in range(KC):
        pst, pkey = T["pst"].next()
        def tr(e, kc=kc, pst=pst):
            ins = None
            for s in range(nsub):
                ins = e.transpose(out=pst[:, s * 128:(s + 1) * 128],
                                  in_=hn[:, s, kc * 128:(kc + 1) * 128], identity=T["ident"][:])
            return ins
        ph.op("pe", tr, reads=[hnk[s] for s in range(nsub)] + ["ident"], writes=[pkey])
        eng = "dve" if kc % 2 == 0 else "act"
        if eng == "dve":
            ph.op("dve", lambda e, kc=kc, pst=pst: e.tensor_scalar(
                out=hT[:, kc, 0:nsub * 128], in0=pst[:, 0:nsub * 128], scalar1=gT[:, kc:kc + 1],
                scalar2=None, op0=ALU.mult), reads=[pkey, gkey], writes=[(hkey, kc)])
        else:
            ph.op("act", lambda e, kc=kc, pst=pst: e.activation(
                out=hT[:, kc, 0:nsub * 128], in_=pst[:, 0:nsub * 128], func=AF.Copy,
                scale=gT[:, kc:kc + 1]), reads=[pkey, gkey], writes=[(hkey, kc)])


def load_w(ph, T, wv, c0, ncol, k0, nk, q="sp"):
    wb, wkey = T["wbuf"].next()
    def f(e, inc, wb=wb):
        inc(e.dma_start(out=wb[:, 0:nk, 0:ncol], in_=wv[:, k0:k0 + nk, c0:c0 + ncol]))
    ph.dma(q, f, 1, "s_" + wkey, reads=[], writes=[wkey])
    return wb, wkey


def evac(ph, T, out, okey, ps, pkey, scale=None, extra_reads=()):
    T["ev"] += 1
    if T["ev"] % 2 == 0:
        if scale is None:
            ph.op("act", lambda e: e.activation(out=out, in_=ps, func=AF.Copy),
                  reads=[pkey] + list(extra_reads), writes=[okey])
        elif isinstance(scale, float):
            ph.op("act", lambda e: e.activation(out=out, in_=ps, func=AF.Copy, scale=scale),
                  reads=[pkey] + list(extra_reads), writes=[okey])
        else:
            ph.op("act", lambda e: e.activation(out=out, in_=ps, func=AF.Copy, scale=scale),
                  reads=[pkey] + list(extra_reads), writes=[okey])
    else:
        if scale is None:
            ph.op("dve", lambda e: e.tensor_copy(out=out, in_=ps),
                  reads=[pkey] + list(extra_reads), writes=[okey])
        else:
            ph.op("dve", lambda e: e.tensor_scalar(out=out, in0=ps, scalar1=scale, scalar2=None,
                                                   op0=ALU.mult),
                  reads=[pkey] + list(extra_reads), writes=[okey])


def phase_A(P, NT, dr):
    nc = P.nc
    NTOK = 128 + 512 * NT
    ph = Phase(P)
    with ExitStack() as st:
        P.uid = getattr(P, "uid", 0) + 1
        def sb(name, shape, dt, u=P.uid):
            return st.enter_context(nc.sbuf_tensor("%s_u%d" % (name, u), shape, dt))
        def pp(name, shape, dt, u=P.uid):
            return st.enter_context(nc.psum_tensor("%s_u%d" % (name, u), shape, dt))
        xts = [sb("a_xt%d" % i, [128, 4, D], F32) for i in range(2)]
        hn = sb("a_hn", [128, 4, D], BF16)
        junk = sb("a_junk", [128, D], BF16)
        hTs = [sb("a_hT%d" % i, [128, KC, 512], BF16) for i in range(2)]
        wbs = [sb("a_w%d" % i, [128, KC, 512], BF16) for i in range(2)]
        uT = sb("a_uT", [128, 8, 528], F32)
        tA = sb("a_tA", [128, 528], F32)
        tB = sb("a_tB", [128, 528], F32)
        pm = sb("a_pm", [128, 8, 512], BF16)
        stg = [sb("a_stg%d" % i, [128, 512], BF16) for i in range(3)]
        vst = [sb("a_vst%d" % i, [128, 512], BF16) for i in range(2)]
        ss = sb("a_ss", [128, 4], F32)
        rstd = sb("a_rstd", [128, 4], F32)
        gpre = sb("a_gpre", [128, KC], F32)
        gflag = sb("a_gflag", [128, KC], F32)
        flag = sb("a_flag", [128, 1], F32)
        epsb = sb("a_eps", [128, 1], F32)
        pscale = sb("a_pscale", [128, 8], F32)
        invc = sb("a_invc", [128, 4, 16], F32)
        wpool = sb("a_wpool", [128, 8, 256], BF16)
        ident = sb("a_ident", [128, 128], BF16)
        psts = [pp("a_pst%d" % i, [128, 1024], BF16) for i in range(2)]
        pss = [pp("a_ps%d" % i, [128, 512], F32) for i in range(4)]
        T = dict(ss=ss, rstd=rstd, junk=junk, hn=hn, ident=ident, epsb=epsb, ev=0,
                 pst=Rot([(psts[i], "pst%d" % i) for i in range(2)]),
                 wbuf=Rot([(wbs[i], "w%d" % i) for i in range(2)]))
        psr = Rot([(pss[i], "ps%d" % i) for i in range(4)])
        stgr = Rot([(stg[i], "stg%d" % i) for i in range(3)])
        vstr = Rot([(vst[i], "vst%d" % i) for i in range(2)])

        def ld(dst, src, key, sem):
            ph.dma("sp", lambda e, inc: inc(e.dma_start(out=dst, in_=src)), 1, sem, writes=[key])
        ld(gpre[:], dr["gpre"], "gpre", "s_c0")
        ld(flag[:], dr["flag"], "flag", "s_c1")
        ld(pscale[:], dr["pscale"], "pscale", "s_c2")
        ld(invc[:], dr["invc"], "invc", "s_c3")
        ld(wpool[:], dr["wpool"], "wpool", "s_c4")
        ld(ident[:], dr["ident"], "ident", "s_c5")
        ph.op("dve", lambda e: e.memset(epsb[:], EPS), writes=["epsb"])
        ph.op("dve", lambda e: e.tensor_scalar(out=gflag[:], in0=gpre[:], scalar1=flag[:, 0:1],
                                               scalar2=None, op0=ALU.mult),
              reads=["gpre", "flag"], writes=["gflag"])
        ph.op("pool", lambda e: e.memset(uT[:], 0.0), writes=[("uT", j) for j in range(8)])

        win = dr["w_in"].rearrange("(kc p) n -> p kc n", p=128)
        tl = tiles_of(NT)
        def tileA(ti, tok0, nsub):
            ntok = nsub * 128
            xt, xkey = xts[ti % 2], "xt%d" % (ti % 2)
            hT, hkey = hTs[ti % 2], "hT%d" % (ti % 2)
            ph.dma("sp", lambda e, inc, xt=xt, tok0=tok0, nsub=nsub: inc(e.dma_start(
                out=xt[:, 0:nsub, :],
                in_=dr["x"][tok0:tok0 + ntok, :].rearrange("(s p) d -> p s d", p=128))),
                1, "s_" + xkey, writes=[xkey])
            norm_transpose(ph, T, xt, xkey, nsub, gflag if ti == 0 else gpre,
                           "gflag" if ti == 0 else "gpre", hT, hkey, "a")
            hreads = [(hkey, kc) for kc in range(KC)]
            for c in range(8):
                wb, wkey = load_w(ph, T, win, c * 512, 512, 0, KC)
                if c < 4 or c >= 6:
                    for m in range(4):
                        ps, pkey = psr.next()
                        def mm(e, ps=ps, wb=wb, m=m, hT=hT):
                            ins = None
                            for kc in range(KC):
                                ins = e.matmul(ps[:, 0:ntok], wb[:, kc, m * 128:(m + 1) * 128],
                                               hT[:, kc, 0:ntok], start=(kc == 0), stop=(kc == KC - 1))
                            return ins
                        ph.op("pe", mm, reads=hreads + [wkey], writes=[pkey])
                        if c < 4:
                            sg, skey = stgr.next()
                            hh = (c % 2) * 4 + m
                            evac(ph, T, sg[:, 0:ntok], skey, ps[:, 0:ntok], pkey,
                                 scale=(0.125 if c < 2 else None))
                            dst = dr["qT"] if c < 2 else dr["kT"]
                            ph.dma("sp", lambda e, inc, sg=sg, dst=dst, hh=hh: inc(e.dma_start(
                                out=dst[hh, :, tok0:tok0 + ntok], in_=sg[:, 0:ntok])),
                                1, "s_" + skey, reads=[skey])
                        else:
                            j = (c - 6) * 4 + m
                            evac(ph, T, uT[:, j, 16:16 + ntok], ("uT", j), ps[:, 0:ntok], pkey)
                else:
                    for s in range(nsub):
                        ps, pkey = psr.next()
                        def mm(e, ps=ps, wb=wb, s=s, hT=hT):
                            ins = None
                            for kc in range(KC):
                                ins = e.matmul(ps[:, :], hT[:, kc, s * 128:(s + 1) * 128],
                                               wb[:, kc, :], start=(kc == 0), stop=(kc == KC - 1))
                            return ins
                        ph.op("pe", mm, reads=hreads + [wkey], writes=[pkey])
                        vs, vkey = vstr.next()
                        evac(ph, T, vs[:, :], vkey, ps[:, :], pkey)
                        r0 = tok0 + s * 128
                        h0 = (c - 4) * 4
                        ph.dma("sp", lambda e, inc, vs=vs, r0=r0, h0=h0: inc(e.dma_start(
                            out=dr["v"][h0:h0 + 4, r0:r0 + 128, :].rearrange("h p d -> p h d"),
                            in_=vs[:, :].rearrange("p (h d) -> p h d", h=4))),
                            1, "s_" + vkey, reads=[vkey])
            W = 16 + ntok
            for j in range(8):
                g = j // 2
                cur = uT[:, j, :]
                src, skey = cur, ("uT", j)
                bufs = [(tA, "tA"), (tB, "tB")]
                sh, lo = 1, 1
                for step in range(g + 1):
                    dst, dkey = bufs[step % 2]
                    ph.op("pool", lambda e, dst=dst, src=src, sh=sh, lo=lo: e.tensor_tensor(
                        out=dst[:, lo:W], in0=src[:, lo:W], in1=src[:, lo - sh:W - sh], op=ALU.add),
                        reads=[skey], writes=[dkey])
                    src, skey = dst, dkey
                    sh *= 2
                    lo += sh
                w = 2 ** (g + 1)
                ph.op("dve", lambda e, src=src, cur=cur, j=j, w=w: e.scalar_tensor_tensor(
                    out=pm[:, j, 0:ntok], in0=src[:, 16:W], scalar=1.0 / w, in1=cur[:, 16:W],
                    op0=ALU.mult, op1=ALU.subtract), reads=[skey, ("uT", j)], writes=[("pm", j)])
                if ti == 1:
                    fx, fkey = bufs[(g + 1) % 2]
                    ph.op("pool", lambda e, fx=fx, src=src, g=g: e.tensor_tensor(
                        out=fx[:, 0:16], in0=src[:, 16:32], in1=invc[:, g, :], op=ALU.mult),
                        reads=[skey, "invc"], writes=[fkey])
                    ph.op("pool", lambda e, fx=fx, cur=cur, j=j: e.tensor_tensor(
                        out=pm[:, j, 0:16], in0=fx[:, 0:16], in1=cur[:, 16:32], op=ALU.subtract),
                        reads=[fkey, ("uT", j)], writes=[("pm", j)])
                ph.op("pool", lambda e, cur=cur: e.tensor_copy(out=cur[:, 0:16], in_=cur[:, ntok:ntok + 16]),
                      reads=[("uT", j)], writes=[("uT", j)])
            for g in range(4):
                for m in range(2):
                    ps, pkey = psr.next()
                    def mm(e, ps=ps, g=g, m=m):
                        ins = None
                        for kc in range(2):
                            ins = e.matmul(ps[:, 0:ntok], wpool[:, 2 * g + kc, m * 128:(m + 1) * 128],
                                           pm[:, 2 * g + kc, 0:ntok], start=(kc == 0), stop=(kc == 1))
                        return ins
                    ph.op("pe", mm, reads=[("pm", 2 * g), ("pm", 2 * g + 1), "wpool"], writes=[pkey])
                    sg, skey = stgr.next()
                    jj = 2 * g + m
                    evac(ph, T, sg[:, 0:ntok], skey, ps[:, 0:ntok], pkey, scale=pscale[:, jj:jj + 1],
                         extra_reads=["pscale"])
                    ph.dma("sp", lambda e, inc, sg=sg, jj=jj: inc(e.dma_start(
                        out=dr["catp"][jj, :, tok0:tok0 + ntok], in_=sg[:, 0:ntok])),
                        1, "s_" + skey, reads=[skey])
        for ti, (tok0, nsub) in enumerate(tl):
            tileA(ti, tok0, nsub)
        ph.emit()


def phase_B(P, NT, dr, layer, acols):
    nc = P.nc
    NTOK = 128 + 512 * NT
    SO = 512 * NT
    NBH = 4 * NT
    lam_init = 0.8 - 0.6 * math.exp(-0.3 * layer)
    ph = Phase(P)
    with ExitStack() as st:
        P.uid = getattr(P, "uid", 0) + 1
        def sb(name, shape, dt, u=P.uid):
            return st.enter_context(nc.sbuf_tensor("%s_u%d" % (name, u), shape, dt))
        def pp(name, shape, dt, u=P.uid):
            return st.enter_context(nc.psum_tensor("%s_u%d" % (name, u), shape, dt))
        Ks = [sb("b_K%d" % i, [128, 2 * SO], BF16) for i in range(2)]
        Vs = [sb("b_V%d" % i, [128, 2 * NBH, 130], BF16) for i in range(2)]
        Qs = [sb("b_Q%d" % i, [128, NTOK], BF16) for i in range(2)]
        Es = [sb("b_E%d" % i, [128, 512], BF16) for i in range(4)]
        ostg = [sb("b_os%d" % i, [128, 512], BF16) for i in range(2)]
        osb = [sb("b_o%d" % i, [128, 128], F32) for i in range(2)]
        tsb = [sb("b_t%d" % i, [128, 128], F32) for i in range(2)]
        onb = [sb("b_on%d" % i, [128, 128], BF16) for i in range(2)]
        junk = sb("b_junk", [128, 128], BF16)
        sm = sb("b_sm", [128, 2], F32)
        rs = sb("b_rs", [128, 2], F32)
        c2 = sb("b_c2", [128, 1], F32)
        ss = sb("b_ss", [128, 1], F32)
        rstd = sb("b_rstd", [128, 1], F32)
        epsb = sb("b_eps", [128, 1], F32)
        lamb = sb("b_lamb", [128, 4, 64], F32)
        lprod = sb("b_lprod", [128, 2, 64], F32)
        lsum = sb("b_lsum", [128, 2], F32)
        neglam = sb("b_neglam", [128, 1], F32)
        ghead = sb("b_ghead", [128, 1024], F32)
        alibi = sb("b_alibi", [128, dr["alibi"].shape[1]], F32)
        mask = sb("b_mask", [128, 4, 512], BF16)
        ident = sb("b_ident", [128, 128], BF16)
        flag = sb("b_flag", [128, 1], F32)
        Sps = [pp("b_S%d" % i, [128, 512], F32) for i in range(4)]
        accb = [pp("b_acc%d" % i, [128, 512], F32) for i in range(3)]
        ptr = pp("b_ptr", [128, 1024], BF16)

        def ld(dst, src, key, sem):
            ph.dma("sp", lambda e, inc: inc(e.dma_start(out=dst, in_=src)), 1, sem, writes=[key])
        ld(lamb[:], dr["lamb"], "lamb", "s_c0")
        ld(ghead[:], dr["ghead"], "ghead", "s_c1")
        ld(alibi[:], dr["alibi"], "alibi", "s_c2")
        ld(mask[:], dr["mask"], "mask", "s_c3")
        ld(ident[:], dr["ident"], "ident", "s_c4")
        ld(flag[:], dr["flag"], "flag", "s_c5")
        ph.op("dve", lambda e: e.memset(epsb[:], EPS), writes=["epsb"])
        ph.op("dve", lambda e: e.tensor_tensor(out=lprod[:, 0, :], in0=lamb[:, 0, :], in1=lamb[:, 1, :], op=ALU.mult),
              reads=["lamb"], writes=["lprod0"])
        ph.op("dve", lambda e: e.tensor_tensor(out=lprod[:, 1, :], in0=lamb[:, 2, :], in1=lamb[:, 3, :], op=ALU.mult),
              reads=["lamb"], writes=["lprod1"])
        ph.op("dve", lambda e: e.reduce_sum(out=lsum[:, 0:1], in_=lprod[:, 0, :], axis=AX.X),
              reads=["lprod0"], writes=["lsum"])
        ph.op("dve", lambda e: e.reduce_sum(out=lsum[:, 1:2], in_=lprod[:, 1, :], axis=AX.X),
              reads=["lprod1", "lsum"], writes=["lsum"])
        ph.op("act", lambda e: e.activation(out=lsum[:], in_=lsum[:], func=AF.Exp), reads=["lsum"], writes=["lsum"])
        ph.op("dve", lambda e: e.tensor_tensor(out=neglam[:], in0=lsum[:, 1:2], in1=lsum[:, 0:1], op=ALU.subtract),
              reads=["lsum"], writes=["neglam"])
        ph.op("dve", lambda e: e.tensor_scalar(out=neglam[:], in0=neglam[:], scalar1=-lam_init, scalar2=None, op0=ALU.add),
              reads=["neglam"], writes=["neglam"])
        ph.op("dve", lambda e: e.tensor_scalar(out=ghead[:], in0=ghead[:], scalar1=1.0 - lam_init, scalar2=None, op0=ALU.mult),
              reads=["ghead"], writes=["ghead"])
        for i in range(2):
            ph.op("pool", lambda e, i=i: e.memset(Vs[i][:, :, 128:130], 1.0), writes=["V%d" % i])
            ph.op("pool", lambda e, i=i: e.tensor_scalar(out=Vs[i][:, 0:NBH, 128:130], in0=Vs[i][:, 0:NBH, 128:130],
                                                         scalar1=flag[:, 0:1], scalar2=None, op0=ALU.mult),
                  reads=["flag", "V%d" % i], writes=["V%d" % i])

        Sr = Rot([(Sps[i], "S%d" % i) for i in range(4)])
        Er = Rot([(Es[i], "E%d" % i) for i in range(4)])
        osr = Rot([(ostg[i], "os%d" % i) for i in range(2)])
        nrm = [0]
        def acc(mp, s):
            a = mp * 4 + s
            return accb[a // 3][:, (a % 3) * 130:(a % 3) * 130 + 130], "acc%d" % (a // 3)
        tl = tiles_of(NT)
        for h in range(NH):
            K, Kkey = Ks[h % 2], "K%d" % (h % 2)
            V, Vkey = Vs[h % 2], "V%d" % (h % 2)
            Q, Qkey = Qs[h % 2], "Q%d" % (h % 2)
            def ldk(e, inc, K=K, h=h):
                inc(e.dma_start(out=K[:, 0:SO], in_=dr["kTp"][h]))
                inc(e.dma_start(out=K[:, SO:2 * SO], in_=dr["kT"][h, :, 128:NTOK]))
            ph.dma("sp", ldk, 2, "s_" + Kkey, writes=[Kkey])
            def ldv(e, inc, V=V, h=h):
                inc(e.dma_start(out=V[:, 0:NBH, 0:128],
                                in_=dr["vp"][h].rearrange("(e p) d -> p e d", p=128)))
                inc(e.dma_start(out=V[:, NBH:2 * NBH, 0:128],
                                in_=dr["v"][h, 128:NTOK, :].rearrange("(e p) d -> p e d", p=128)))
            ph.dma("sp", ldv, 2, "s_" + Vkey, writes=[Vkey])
            ph.dma("sp", lambda e, inc, Q=Q, h=h: inc(e.dma_start(out=Q[:, :], in_=dr["qT"][h, :, :])),
                   1, "s_" + Qkey, writes=[Qkey])
            ph.op("pool", lambda e, V=V: e.tensor_scalar(out=V[:, 0:NBH, 0:128], in0=V[:, 0:NBH, 0:128],
                                                         scalar1=flag[:, 0:1], scalar2=None, op0=ALU.mult),
                  reads=["flag", Vkey], writes=[Vkey])
            back = int((THR / SLOPES[h] - 1) // 128) + 1
            for ti, (tok0, nsub_t) in enumerate(tl):
                qb_t = NBH - 1 if ti == 0 else NBH + 4 * (ti - 1)
                if nsub_t == 4 and h >= 2:
                    groups = [(tok0, 4, qb_t)]
                else:
                    groups = [(tok0 + 128 * s, 1, qb_t + s) for s in range(nsub_t)]
                og, ogkey = osr.next()
                def do_group(gi, q0, nsub, qb0, h=h, K=K, V=V, Q=Q, Kkey=Kkey, Vkey=Vkey, Qkey=Qkey,
                             back=back, og=og, ogkey=ogkey):
                    nq = nsub * 128
                    lo, hi = max(0, qb0 - back), qb0 + nsub
                    first = True
                    for eb in range(lo, hi):
                        r = eb - qb0
                        col = acols[(h, nsub, r)]
                        Sp = []
                        for mp in range(2):
                            S, Skey = Sr.next()
                            def qk(e, S=S, mp=mp, eb=eb, r=r):
                                ins = e.matmul(S[:, 0:nq], K[mp * 64:(mp + 1) * 64, eb * 128:(eb + 1) * 128],
                                               Q[mp * 64:(mp + 1) * 64, q0:q0 + nq], start=True, stop=(r < 0))
                                if r >= 0:
                                    ins = e.matmul(S[:, 0:nq], ident[:, :], mask[:, r, 0:nq], start=False, stop=True)
                                return ins
                            ph.op("pe", qk, reads=[Kkey, Qkey, "mask", "ident"], writes=[Skey])
                            Sp.append((S, Skey))
                        Ep = []
                        for mp in range(2):
                            E, Ekey = Er.next()
                            S, Skey = Sp[mp]
                            ph.op("act", lambda e, E=E, S=S, col=col: e.activation(
                                out=E[:, 0:nq], in_=S[:, 0:nq], func=AF.Exp, bias=alibi[:, col:col + 1], scale=1.0),
                                reads=[Skey, "alibi"], writes=[Ekey])
                            Ep.append((E, Ekey))
                        def pv(e, Ep=Ep, eb=eb, first=first):
                            ins = None
                            for mp in range(2):
                                for s in range(nsub):
                                    a = mp * 4 + s
                                    ap, _ = acc(mp, s)
                                    st_ = first and (a % 3 == 0 or (nsub == 1 and True))
                                    ins = e.matmul(ap[:, 0:129], Ep[mp][0][:, s * 128:(s + 1) * 128], V[:, eb, 0:129],
                                                   start=st_, stop=(eb == hi - 1), skip_group_check=True)
                            return ins
                        ph.op("pe", pv, reads=[Ep[0][1], Ep[1][1], Vkey], writes=["acc0", "acc1", "acc2"])
                        first = False
                    for s in range(nsub):
                        k = nrm[0] % 2
                        nrm[0] += 1
                        o, okey = osb[k], "o%d" % k
                        t, tkey = tsb[k], "t%d" % k
                        on, onkey = onb[k], "on%d" % k
                        a1, a1k = acc(0, s)
                        a2, a2k = acc(1, s)
                        ph.op("dve", lambda e, a1=a1: e.tensor_scalar_max(out=sm[:, 0:1], in0=a1[:, 128:129], scalar1=1e-30),
                              reads=[a1k], writes=["sm"])
                        ph.op("dve", lambda e, a2=a2: e.tensor_scalar_max(out=sm[:, 1:2], in0=a2[:, 128:129], scalar1=1e-30),
                              reads=[a2k, "sm"], writes=["sm"])
                        ph.op("dve", lambda e: e.reciprocal(out=rs[:], in_=sm[:]), reads=["sm"], writes=["rs"])
                        ph.op("dve", lambda e: e.tensor_tensor(out=c2[:], in0=rs[:, 1:2], in1=neglam[:], op=ALU.mult),
                              reads=["rs", "neglam"], writes=["c2"])
                        ph.op("dve", lambda e, t=t, a2=a2: e.tensor_scalar(out=t[:], in0=a2[:, 0:128], scalar1=c2[:, 0:1],
                                                                           scalar2=None, op0=ALU.mult),
                              reads=[a2k, "c2"], writes=[tkey])
                        ph.op("dve", lambda e, o=o, t=t, a1=a1: e.scalar_tensor_tensor(
                            out=o[:], in0=a1[:, 0:128], scalar=rs[:, 0:1], in1=t[:], op0=ALU.mult, op1=ALU.add),
                            reads=[a1k, "rs", tkey], writes=[okey])
                        ph.op("act", lambda e, o=o: e.activation(out=junk[:], in_=o[:], func=AF.Square, accum_out=ss[:, 0:1]),
                              reads=[okey], writes=["junk", "ss"])
                        rms_rstd(ph, ss, rstd, 128, "ss", "rstd", 1, epsb)
                        ph.op("dve", lambda e, o=o, on=on, h=h: e.scalar_tensor_tensor(
                            out=on[:], in0=o[:], scalar=rstd[:, 0:1], in1=ghead[:, h * 128:(h + 1) * 128],
                            op0=ALU.mult, op1=ALU.mult), reads=[okey, "rstd", "ghead"], writes=[onkey])
                        c0 = (gi if nsub == 1 else s) * 128
                        ph.op("pe", lambda e, on=on, c0=c0: e.transpose(out=ptr[:, c0:c0 + 128], in_=on[:], identity=ident[:]),
                              reads=[onkey, "ident"], writes=[("ptr", c0)])
                        ph.op("act", lambda e, og=og, c0=c0: e.activation(out=og[:, c0:c0 + 128], in_=ptr[:, c0:c0 + 128], func=AF.Copy),
                              reads=[("ptr", c0)], writes=[ogkey])
                for gi, (q0, nsub, qb0) in enumerate(groups):
                    do_group(gi, q0, nsub, qb0)
                ntok = nsub_t * 128
                ph.dma("sp", lambda e, inc, og=og, h=h, tok0=tok0, ntok=ntok: inc(e.dma_start(
                    out=dr["cat"][h, :, tok0:tok0 + ntok], in_=og[:, 0:ntok])), 1, "s_" + ogkey, reads=[ogkey])
        ph.emit()


def phase_C(P, NT, dr):
    nc = P.nc
    NTOK = 128 + 512 * NT
    ph = Phase(P)
    with ExitStack() as st:
        P.uid = getattr(P, "uid", 0) + 1
        def sb(name, shape, dt, u=P.uid):
            return st.enter_context(nc.sbuf_tensor("%s_u%d" % (name, u), shape, dt))
        def pp(name, shape, dt, u=P.uid):
            return st.enter_context(nc.psum_tensor("%s_u%d" % (name, u), shape, dt))
        xt = sb("c_xt", [128, 4, D], F32)
        y = sb("c_y", [128, 4, D], F32)
        hT = sb("c_hT", [128, KC, 512], BF16)
        catT = hT
        hn = y[:, 0:2, :].bitcast(BF16).rearrange("p a (b d) -> p (a b) d", b=2)
        junk = sb("c_junk", [128, D], BF16)
        actT = sb("c_act", [128, FC, 512], BF16)
        wbs = [sb("c_w%d" % i, [128, 16, 512], BF16) for i in range(2)]
        pre = [sb("c_pre%d" % i, [128, 516], F32) for i in range(2)]
        cva = [sb("c_cv%d" % i, [128, 512], F32) for i in range(2)]
        gg = sb("c_gg", [128, 512], F32)
        ptmp = sb("c_ptmp", [128, 512], F32)
        halo = sb("c_halo", [128, 2 * FC, 2], F32)
        ss = sb("c_ss", [128, 4], F32)
        rstd = sb("c_rstd", [128, 4], F32)
        epsb = sb("c_eps", [128, 1], F32)
        gffn = sb("c_gffn", [128, KC], F32)
        gflag = sb("c_gflag", [128, KC], F32)
        flag = sb("c_flag", [128, 1], F32)
        gpost = sb("c_gpost", [128, D], F32)
        gpost2 = sb("c_gpost2", [128, D], F32)
        cw = sb("c_cw", [128, 4, 2 * FC], F32)
        ident = sb("c_ident", [128, 128], BF16)
        psts = [pp("c_pst%d" % i, [128, 1024], BF16) for i in range(2)]
        pss = [pp("c_ps%d" % i, [128, 512], F32) for i in range(6)]
        T = dict(ss=ss, rstd=rstd, junk=junk, hn=hn, ident=ident, epsb=epsb, ev=0,
                 pst=Rot([(psts[i], "pst%d" % i) for i in range(2)]),
                 wbuf=Rot([(wbs[i], "w%d" % i) for i in range(2)]),
                 hnkeys=[("y", 0), ("y", 0), ("y", 1), ("y", 1)])

        def ld(dst, src, key, sem):
            ph.dma("sp", lambda e, inc: inc(e.dma_start(out=dst, in_=src)), 1, sem, writes=[key])
        ld(gffn[:], dr["gffn"], "gffn", "s_c0")
        ld(flag[:], dr["flag"], "flag", "s_c1")
        ld(gpost[:], dr["gpost"], "gpost", "s_c2")
        ld(gpost2[:], dr["gpost2"], "gpost2", "s_c3")
        ld(cw[:], dr["cw"], "cw", "s_c4")
        ld(ident[:], dr["ident"], "ident", "s_c5")
        ph.op("dve", lambda e: e.memset(epsb[:], EPS), writes=["epsb"])
        ph.op("dve", lambda e: e.tensor_scalar(out=gflag[:], in0=gffn[:], scalar1=flag[:, 0:1],
                                               scalar2=None, op0=ALU.mult),
              reads=["gffn", "flag"], writes=["gflag"])
        ph.op("pool", lambda e: e.memset(halo[:], 0.0), writes=[("halo", i) for i in range(2 * FC)])

        wout = dr["w_out"].rearrange("(kc p) n -> p kc n", p=128)
        wup = dr["w_up"].rearrange("(kc p) n -> p kc n", p=128)
        wdn = dr["w_down"].rearrange("(kc p) n -> p kc n", p=128)

        def post_norm(nsub, gtile, gkey):
            for s in range(nsub):
                ph.op("act", lambda e, s=s: e.activation(out=junk[:], in_=y[:, s, :], func=AF.Square,
                                                         accum_out=ss[:, s:s + 1]),
                      reads=[("y", s)], writes=["junk", "ss"])
            rms_rstd(ph, ss, rstd, D, "ss", "rstd", nsub, epsb)
            for s in range(nsub):
                ph.op("dve", lambda e, s=s: e.scalar_tensor_tensor(
                    out=y[:, s, :], in0=y[:, s, :], scalar=rstd[:, s:s + 1], in1=gtile[:],
                    op0=ALU.mult, op1=ALU.mult), reads=[("y", s), "rstd", gkey], writes=[("y", s)])
                ph.op("pool", lambda e, s=s: e.tensor_tensor(out=xt[:, s, :], in0=xt[:, s, :], in1=y[:, s, :],
                                                             op=ALU.add),
                      reads=[("y", s), "xt"], writes=["xt"])

        upr = Rot([(pss[i], "ps%d" % i) for i in range(6)])
        tl = tiles_of(NT)
        def tileC(ti, tok0, nsub):
            ntok = nsub * 128
            ph.dma("sp", lambda e, inc, tok0=tok0, nsub=nsub, ntok=ntok: inc(e.dma_start(
                out=xt[:, 0:nsub, :],
                in_=dr["x"][tok0:tok0 + ntok, :].rearrange("(s p) d -> p s d", p=128))),
                1, "s_xt", writes=["xt"])
            def ldcat(e, inc, tok0=tok0, ntok=ntok):
                inc(e.dma_start(out=catT[:, 0:8, 0:ntok],
                                in_=dr["cat"][:, :, tok0:tok0 + ntok].rearrange("h p t -> p h t")))
                inc(e.dma_start(out=catT[:, 8:16, 0:ntok],
                                in_=dr["catp"][:, :, tok0:tok0 + ntok].rearrange("h p t -> p h t")))
            ph.dma("sp", ldcat, 2, "s_cat", writes=[("hT", kc) for kc in range(KC)])
            for c in range(4):
                wb, wkey = load_w(ph, T, wout, c * 512, 512, 0, KC)
                for s in range(nsub):
                    ps, pkey = pss[s], "ps%d" % s
                    def mm(e, ps=ps, wb=wb, s=s):
                        ins = None
                        for kc in range(KC):
                            ins = e.matmul(ps[:, :], catT[:, kc, s * 128:(s + 1) * 128], wb[:, kc, :],
                                           start=(kc == 0), stop=(kc == KC - 1))
                        return ins
                    ph.op("pe", mm, reads=[("hT", kc) for kc in range(KC)] + [wkey], writes=[pkey])
                    evac(ph, T, y[:, s, c * 512:(c + 1) * 512], ("y", s), ps[:, :], pkey)
            post_norm(nsub, gpost, "gpost")
            norm_transpose(ph, T, xt, "xt", nsub, gflag if ti == 0 else gffn,
                           "gflag" if ti == 0 else "gffn", hT, "hT", "c")
            hreads = [("hT", kc) for kc in range(KC)]
            for cg in range(22):
                wb, wkey = T["wbuf"].next()
                def ldgv(e, inc, wb=wb, cg=cg):
                    inc(e.dma_start(out=wb[:, :, 0:256], in_=wup[:, :, cg * 256:(cg + 1) * 256]))
                    inc(e.dma_start(out=wb[:, :, 256:512], in_=wup[:, :, DFF + cg * 256:DFF + (cg + 1) * 256]))
                ph.dma("sp", ldgv, 2, "s_" + wkey, writes=[wkey])
                for m in range(2):
                    i = cg * 2 + m
                    res = []
                    for half in range(2):
                        ps, pkey = upr.next()
                        def mm(e, ps=ps, wb=wb, m=m, half=half):
                            ins = None
                            c0 = half * 256 + m * 128
                            for kc in range(KC):
                                ins = e.matmul(ps[:, 0:ntok], wb[:, kc, c0:c0 + 128],
                                               hT[:, kc, 0:ntok], start=(kc == 0), stop=(kc == KC - 1))
                            return ins
                        ph.op("pe", mm, reads=hreads + [wkey], writes=[pkey])
                        ch = half * FC + i
                        pr, prkey = pre[half], "pre%d" % half
                        cv, cvkey = cva[half], "cv%d" % half
                        eng = "dve" if half == 0 else "pool"
                        ph.op("act", lambda e, pr=pr, ps=ps: e.activation(out=pr[:, 2:2 + ntok], in_=ps[:, 0:ntok], func=AF.Copy),
                              reads=[pkey], writes=[prkey])
                        ph.op(eng, lambda e, pr=pr, ch=ch: e.tensor_copy(out=pr[:, 0:2], in_=halo[:, ch, :]),
                              reads=[("halo", ch)], writes=[prkey])
                        ph.op(eng, lambda e, pr=pr, cv=cv, ch=ch: e.tensor_scalar(
                            out=cv[:, 0:ntok], in0=pr[:, 2:2 + ntok], scalar1=cw[:, 2, ch:ch + 1],
                            scalar2=cw[:, 3, ch:ch + 1], op0=ALU.mult, op1=ALU.add),
                            reads=[prkey, "cw"], writes=[cvkey])
                        for tap in (1, 0):
                            if eng == "dve":
                                ph.op(eng, lambda e, pr=pr, cv=cv, ch=ch, tap=tap: e.scalar_tensor_tensor(
                                    out=cv[:, 0:ntok], in0=pr[:, tap:tap + ntok], scalar=cw[:, tap, ch:ch + 1],
                                    in1=cv[:, 0:ntok], op0=ALU.mult, op1=ALU.add),
                                    reads=[prkey, "cw", cvkey], writes=[cvkey])
                            else:
                                ph.op(eng, lambda e, pr=pr, ch=ch, tap=tap: e.tensor_scalar(
                                    out=ptmp[:, 0:ntok], in0=pr[:, tap:tap + ntok], scalar1=cw[:, tap, ch:ch + 1],
                                    scalar2=0.0, op0=ALU.mult, op1=ALU.add),
                                    reads=[prkey, "cw"], writes=["ptmp"])
                                ph.op(eng, lambda e, cv=cv: e.tensor_tensor(
                                    out=cv[:, 0:ntok], in0=cv[:, 0:ntok], in1=ptmp[:, 0:ntok], op=ALU.add),
                                    reads=["ptmp", cvkey], writes=[cvkey])
                        ph.op(eng, lambda e, pr=pr, ch=ch: e.tensor_copy(out=halo[:, ch, :], in_=pr[:, ntok:ntok + 2]),
                              reads=[prkey], writes=[("halo", ch)])
                        res.append((cv, cvkey))
                    ph.op("act", lambda e, cv=res[0][0]: e.activation(out=gg[:, 0:ntok], in_=cv[:, 0:ntok], func=AF.Gelu_apprx_tanh),
                          reads=[res[0][1]], writes=["gg"])
                    ph.op("dve", lambda e, i=i, cv=res[1][0]: e.tensor_tensor(out=actT[:, i, 0:ntok], in0=gg[:, 0:ntok],
                                                                            in1=cv[:, 0:ntok], op=ALU.mult),
                          reads=["gg", res[1][1]], writes=[("act", i)])
            areads = [("act", i) for i in range(FC)]
            for c in range(4):
                for kq in range(4):
                    wb, wkey = load_w(ph, T, wdn, c * 512, 512, kq * 11, 11)
                    for s in range(nsub):
                        ps, pkey = pss[s], "ps%d" % s
                        def mm(e, ps=ps, wb=wb, s=s, kq=kq):
                            ins = None
                            for kc in range(11):
                                ins = e.matmul(ps[:, :], actT[:, kq * 11 + kc, s * 128:(s + 1) * 128], wb[:, kc, :],
                                               start=(kq == 0 and kc == 0), stop=(kq == 3 and kc == 10))
                            return ins
                        ph.op("pe", mm, reads=areads + [wkey], writes=[pkey])
                        if kq == 3:
                            evac(ph, T, y[:, s, c * 512:(c + 1) * 512], ("y", s), ps[:, :], pkey)
            post_norm(nsub, gpost2, "gpost2")
            if dr.get("final") is not None:
                if ti > 0:
                    ph.dma("sp", lambda e, inc, tok0=tok0, nsub=nsub, ntok=ntok: inc(e.dma_start(
                        out=dr["final"][tok0 - 128:tok0 - 128 + ntok, :].rearrange("(s p) d -> p s d", p=128),
                        in_=xt[:, 0:nsub, :])), 1, "s_xo", reads=["xt"])
            else:
                ph.dma("sp", lambda e, inc, tok0=tok0, nsub=nsub, ntok=ntok: inc(e.dma_start(
                    out=dr["xo"][tok0:tok0 + ntok, :].rearrange("(s p) d -> p s d", p=128),
                    in_=xt[:, 0:nsub, :])), 1, "s_xo", reads=["xt"])
        for ti, (tok0, nsub) in enumerate(tl):
            tileC(ti, tok0, nsub)
        ph.emit()


def _dt(nc, name, shape, dt, kind):
    return nc.dram_tensor(name, list(shape), dt, kind=kind).ap()


def build_conv(shapes):
    nc = bass.Bass("TRN2", target_bir_lowering=False)
    with ExitStack() as st:
        P = Prog(nc, st)
        ph = Phase(P)
        for name, shp in shapes.items():
            src = _dt(nc, name, shp, F32, "ExternalInput")
            dst = _dt(nc, name + "_b", shp, BF16, "ExternalOutput")
            rows = shp[0]
            step = max(1, min(rows, (1 << 20) // shp[1]))
            r = 0
            i = 0
            while r < rows:
                n = min(step, rows - r)
                ph.dma("pool", lambda e, inc, src=src, dst=dst, r=r, n=n: inc(e.dma_start(out=dst[r:r + n, :], in_=src[r:r + n, :])),
                       1, "s_cv%d" % (i % 4))
                r += n
                i += 1
        ph.emit()
    return nc


def build_A(NT):
    NTOK = 128 + 512 * NT
    nc = bass.Bass("TRN2", target_bir_lowering=False)
    with ExitStack() as st:
        P = Prog(nc, st)
        dr = dict(
            x=_dt(nc, "x", [NTOK, D], F32, "ExternalInput"),
            w_in=_dt(nc, "w_in", [D, 4096], BF16, "ExternalInput"),
            wpool=_dt(nc, "wpool", [128, 8, 256], BF16, "ExternalInput"),
            gpre=_dt(nc, "gpre", [128, KC], F32, "ExternalInput"),
            flag=_dt(nc, "flag", [128, 1], F32, "ExternalInput"),
            pscale=_dt(nc, "pscale", [128, 8], F32, "ExternalInput"),
            invc=_dt(nc, "invc", [128, 4, 16], F32, "ExternalInput"),
            ident=_dt(nc, "ident", [128, 128], BF16, "ExternalInput"),
            qT=_dt(nc, "qT", [NH, 128, NTOK], BF16, "ExternalOutput"),
            kT=_dt(nc, "kT", [NH, 128, NTOK], BF16, "ExternalOutput"),
            v=_dt(nc, "v", [NTOK, 1024], BF16, "ExternalOutput"),
            catp=_dt(nc, "catp", [8, 128, NTOK], BF16, "ExternalOutput"),
        )
        phase_A(P, NT, dr)
    return nc


def build_B(NT, layer, acols, ncol):
    NTOK = 128 + 512 * NT
    SO = 512 * NT
    nc = bass.Bass("TRN2", target_bir_lowering=False)
    with ExitStack() as st:
        P = Prog(nc, st)
        dr = dict(
            qT=_dt(nc, "qT", [NH, 128, NTOK], BF16, "ExternalInput"),
            kT=_dt(nc, "kT", [NH, 128, NTOK], BF16, "ExternalInput"),
            v=_dt(nc, "v", [NTOK, 1024], BF16, "ExternalInput"),
            kTp=_dt(nc, "kTp", [NH, 128, SO], BF16, "ExternalInput"),
            vp=_dt(nc, "vp", [SO, 1024], BF16, "ExternalInput"),
            lamb=_dt(nc, "lamb", [128, 4, 64], F32, "ExternalInput"),
            ghead=_dt(nc, "ghead", [128, 1024], F32, "ExternalInput"),
            alibi=_dt(nc, "alibi", [128, ncol], F32, "ExternalInput"),
            mask=_dt(nc, "mask", [128, 4, 512], BF16, "ExternalInput"),
            ident=_dt(nc, "ident", [128, 128], BF16, "ExternalInput"),
            flag=_dt(nc, "flag", [128, 1], F32, "ExternalInput"),
            cat=_dt(nc, "cat", [NH, 128, NTOK], BF16, "ExternalOutput"),
        )
        phase_B(P, NT, dr, layer, acols)
    return nc


def build_C(NT):
    NTOK = 128 + 512 * NT
    nc = bass.Bass("TRN2", target_bir_lowering=False)
    with ExitStack() as st:
        P = Prog(nc, st)
        dr = dict(
            x=_dt(nc, "x", [NTOK, D], F32, "ExternalInput"),
            cat=_dt(nc, "cat", [NH, 128, NTOK], BF16, "ExternalInput"),
            catp=_dt(nc, "catp", [8, 128, NTOK], BF16, "ExternalInput"),
            w_out=_dt(nc, "w_out", [D, D], BF16, "ExternalInput"),
            w_up=_dt(nc, "w_up", [D, 2 * DFF], BF16, "ExternalInput"),
            w_down=_dt(nc, "w_down", [DFF, D], BF16, "ExternalInput"),
            gffn=_dt(nc, "gffn", [128, KC], F32, "ExternalInput"),
            flag=_dt(nc, "flag", [128, 1], F32, "ExternalInput"),
            gpost=_dt(nc, "gpost", [128, D], F32, "ExternalInput"),
            gpost2=_dt(nc, "gpost2", [128, D], F32, "ExternalInput"),
            cw=_dt(nc, "cw", [128, 4, 2 * FC], F32, "ExternalInput"),
            ident=_dt(nc, "ident", [128, 128], BF16, "ExternalInput"),
            xo=_dt(nc, "xo", [NTOK, D], F32, "ExternalOutput"),
        )
        phase_C(P, NT, dr)
    return nc


WSPEC = [("w_in", D, 4096), ("w_out", D, D), ("w_up", D, 2 * DFF), ("w_down", DFF, D), ("w_pool", 1024, 256)]


def build_fused(NT, depth, ncores, stop=None):
    NTOK = 128 + 512 * NT
    SO = 512 * NT
    nc = bass.Bass("TRN2", target_bir_lowering=False)
    ident_np, mask_np, acols, alibi_np = const_tables(NT)
    ncol = alibi_np.shape[1]
    allg = [list(range(ncores))]
    pairs = [[2 * i, 2 * i + 1] for i in range(ncores // 2)]
    with ExitStack() as st:
        P = Prog(nc, st)
        ei = lambda name, shape, dt: _dt(nc, name, shape, dt, "ExternalInput")
        x_in = ei("x", [NTOK, D], F32)
        wsh = {n: ei(n, [depth, r // ncores, c], F32) for n, r, c in WSPEC}
        gpre = ei("gpre", [depth, 128, KC], F32)
        gffn = ei("gffn", [depth, 128, KC], F32)
        pscale = ei("pscale", [depth, 128, 8], F32)
        gpost = ei("gpost", [depth, 128, D], F32)
        gpost2 = ei("gpost2", [depth, 128, D], F32)
        ghead = ei("ghead", [depth, 128, 1024], F32)
        lamb = ei("lamb", [depth, 128, 4, 64], F32)
        cw = ei("cw", [depth, 128, 4, 2 * FC], F32)
        flag = ei("flag", [128, 1], F32)
        invc = ei("invc", [128, 4, 16], F32)
        ident = ei("ident", [128, 128], BF16)
        mask = ei("mask", [128, 4, 512], BF16)
        alibi = ei("alibi", [128, ncol], F32)
        out = _dt(nc, "out", [SO, D], F32, "ExternalOutput")
        it = lambda name, shape, dt: nc.dram_tensor(name, list(shape), dt).ap()
        wbs = {(n, l): it("%s_bs%d" % (n, l), [r // ncores, c], BF16) for n, r, c in WSPEC for l in range(depth)}
        wbf = {(n, l): it("%s_bf%d" % (n, l), [r, c], BF16) for n, r, c in WSPEC for l in range(depth)}
        xbuf = it("xbuf", [NTOK, D], F32)
        qT = it("qT", [NH * 128, NTOK], BF16)
        kT = it("kT", [NH * 128, NTOK], BF16)
        kTg = it("kTg", [2 * NH * 128, NTOK], BF16)
        v = it("v", [NH * NTOK, 128], BF16)
        vg = it("vg", [2 * NH * NTOK, 128], BF16)
        v3 = v.rearrange("(h t) d -> h t d", h=NH)
        catp = it("catp", [8 * 128, NTOK], BF16)
        cat = it("cat", [NH * 128, NTOK], BF16)

        def gather_weights(l):
            ph = Phase(P)
            for n, r, c in WSPEC:
                rows = r // ncores
                step = max(1, min(rows, (1 << 20) // c))
                r0, i = 0, 0
                keys = []
                while r0 < rows:
                    nn = min(step, rows - r0)
                    ph.dma("pool", lambda e, inc, n=n, r0=r0, nn=nn: inc(e.dma_start(
                        out=wbs[(n, l)][r0:r0 + nn, :], in_=wsh[n][l, r0:r0 + nn, :])),
                        1, "s_cv%d" % (i % 4), writes=[(n, "bs", i)])
                    keys.append((n, "bs", i))
                    r0 += nn
                    i += 1
                ph.dma("pool", lambda e, inc, n=n: inc(e.collective_compute(
                    "AllGather", ALU.bypass, replica_groups=allg, ins=[wbs[(n, l)][:, :]], outs=[wbf[(n, l)][:, :]])),
                    1, "s_cc", reads=keys, writes=[(n, "bf")], inc=1)
            ph.emit()
        for l in range(depth):
            gather_weights(l)
        if stop == "W":
            return nc, (ident_np, mask_np, alibi_np)

        r3 = lambda ap: ap.rearrange("(h p) t -> h p t", p=128)
        for l in range(depth):
            drA = dict(x=(x_in if l == 0 else xbuf), w_in=wbf[("w_in", l)],
                       wpool=wbf[("w_pool", l)].rearrange("(g p) o -> p g o", p=128),
                       gpre=gpre[l], flag=flag, pscale=pscale[l], invc=invc, ident=ident,
                       qT=r3(qT), kT=r3(kT), v=v3, catp=r3(catp))
            phase_A(P, NT, drA)
            if stop == "A":
                break
            ph = Phase(P)
            for h in range(NH):
                ph.dma("pool", lambda e, inc, h=h: inc(e.collective_compute(
                    "AllGather", ALU.bypass, replica_groups=pairs, ins=[kT[h * 128:(h + 1) * 128, :]],
                    outs=[kTg[h * 256:(h + 1) * 256, :]])), 1, "s_cc", inc=1)
                ph.dma("pool", lambda e, inc, h=h: inc(e.collective_compute(
                    "AllGather", ALU.bypass, replica_groups=pairs, ins=[v[h * NTOK:(h + 1) * NTOK, :]],
                    outs=[vg[h * 2 * NTOK:(h + 1) * 2 * NTOK, :]])), 1, "s_cc", inc=1)
            ph.emit()
            if stop == "G":
                break
            drB = dict(qT=r3(qT), kT=r3(kT), v=v3,
                       kTp=[kTg[h * 256:h * 256 + 128, 128:NTOK] for h in range(NH)],
                       vp=[vg[h * 2 * NTOK + 128:h * 2 * NTOK + NTOK, :] for h in range(NH)],
                       lamb=lamb[l], ghead=ghead[l], alibi=alibi, mask=mask, ident=ident, flag=flag, cat=r3(cat))
            phase_B(P, NT, drB, l, acols)
            if stop == "B":
                break
            drC = dict(x=(x_in if l == 0 else xbuf), cat=r3(cat), catp=r3(catp), w_out=wbf[("w_out", l)],
                       w_up=wbf[("w_up", l)], w_down=wbf[("w_down", l)], gffn=gffn[l], flag=flag,
                       gpost=gpost[l], gpost2=gpost2[l], cw=cw[l], ident=ident, xo=xbuf,
                       final=(out if l == depth - 1 else None))
            phase_C(P, NT, drC)
    return nc, (ident_np, mask_np, alibi_np)


def run_fused(inp, NT, depth, ncores, stop=None):
    SO = 512 * NT
    NTOK = 128 + SO
    cores = list(range(ncores))
    x = np.asarray(inp["x"], np.float32)
    nc, (ident, mask, alibi) = build_fused(NT, depth, ncores, stop)
    L = range(depth)
    f32 = lambda a: np.asarray(a, np.float32)
    common = dict(
        gpre=np.stack([_pk(inp["g_mix_pre"][l], KC) for l in L]),
        gffn=np.stack([_pk(inp["g_ffn_pre"][l], KC) for l in L]),
        pscale=np.stack([_pk(inp["pool_scale"][l], 8) for l in L]),
        gpost=np.stack([_bc(f32(inp["g_mix_post"][l])) for l in L]),
        gpost2=np.stack([_bc(f32(inp["g_ffn_post"][l])) for l in L]),
        ghead=np.stack([_bc(f32(inp["g_head"][l])) for l in L]),
        lamb=np.stack([np.stack([_bc(f32(inp[k][l])) for k in ("lam_q1", "lam_k1", "lam_q2", "lam_k2")], axis=1) for l in L]),
        cw=np.stack([np.stack([_pk(inp["conv_w"][l][0], 2 * FC), _pk(inp["conv_w"][l][1], 2 * FC),
                               _pk(inp["conv_w"][l][2], 2 * FC), _pk(inp["conv_b"][l], 2 * FC)], axis=1) for l in L]),
        ident=ident, mask=mask, alibi=alibi)
    wfull = {}
    for n, r, c in WSPEC:
        wfull[n] = f32(inp[n])[:depth].reshape(depth, r, c)
    maps = []
    for c in cores:
        b, r = divmod(c, 2)
        xe = np.zeros((NTOK, D), np.float32)
        xe[128:] = x[b, r * SO:(r + 1) * SO]
        if r == 1:
            xe[:128] = x[b, SO - 128:SO]
        t = np.arange(16, dtype=np.float32)
        iv = np.stack([1.0 / np.minimum(t + 1.0, float(w)) if r == 0 else np.full(16, 1.0 / w, np.float32)
                       for w in (2, 4, 8, 16)], axis=0).astype(np.float32)
        m = dict(common)
        m["x"] = xe
        m["flag"] = np.full((128, 1), float(r), np.float32)
        m["invc"] = np.ascontiguousarray(np.broadcast_to(iv[None], (128, 4, 16)))
        for n, rr, cc in WSPEC:
            rows = rr // ncores
            m[n] = np.ascontiguousarray(wfull[n][:, c * rows:(c + 1) * rows, :])
        maps.append(m)
    res = run_bass_kernel_spmd(nc, maps, core_ids=cores).results
    out = np.empty_like(x)
    for c in cores:
        b, r = divmod(c, 2)
        out[b, r * SO:(r + 1) * SO] = np.asarray(res[c]["out"])
    return out


def _pk(v, n):
    return np.ascontiguousarray(np.asarray(v, np.float32).reshape(n, 128).T)


def _bc(v):
    return np.ascontiguousarray(np.broadcast_to(np.asarray(v, np.float32)[None, :], (128, v.shape[-1])))


def run_module(inp, NT, depth, ncores):
    SO = 512 * NT
    NTOK = 128 + SO
    cores = list(range(ncores))
    x = np.asarray(inp["x"], np.float32)
    ident, mask, acols, alibi = const_tables(NT)
    wnames = ["w_in", "w_out", "w_up", "w_down", "w_pool"]
    wb = {}
    for l in range(depth):
        shapes, maps = {}, [dict() for _ in cores]
        for n in wnames:
            w = np.asarray(inp[n][l], np.float32)
            w = w.reshape(-1, w.shape[-1])
            rows = w.shape[0] // ncores
            shapes[n] = [rows, w.shape[1]]
            for c in cores:
                maps[c][n] = np.ascontiguousarray(w[c * rows:(c + 1) * rows])
        res = run_bass_kernel_spmd(build_conv(shapes), maps, core_ids=cores)
        for n in wnames:
            wb[(n, l)] = np.concatenate([np.asarray(res.results[c][n + "_b"]) for c in cores], axis=0)
    xs = []
    for c in cores:
        b, r = divmod(c, 2)
        xe = np.zeros((NTOK, D), np.float32)
        xe[128:] = x[b, r * SO:(r + 1) * SO]
        if r == 1:
            xe[:128] = x[b, SO - 128:SO]
        xs.append(xe)
    flags = [np.full((128, 1), float(c % 2), np.float32) for c in cores]
    invcs = []
    for c in cores:
        t = np.arange(16, dtype=np.float32)
        iv = np.stack([1.0 / np.minimum(t + 1.0, float(w)) if c % 2 == 0 else np.full(16, 1.0 / w, np.float32)
                       for w in (2, 4, 8, 16)], axis=0).astype(np.float32)
        invcs.append(np.ascontiguousarray(np.broadcast_to(iv[None], (128, 4, 16))))
    ncA = build_A(NT)
    ncC = build_C(NT)
    for l in range(depth):
        wp = wb[("w_pool", l)].reshape(4, 2, 128, 256).transpose(2, 0, 1, 3).reshape(128, 8, 256)
        mapsA = [dict(x=xs[c], w_in=wb[("w_in", l)], wpool=np.ascontiguousarray(wp),
                      gpre=_pk(inp["g_mix_pre"][l], KC), flag=flags[c],
                      pscale=_pk(inp["pool_scale"][l], 8), invc=invcs[c], ident=ident) for c in cores]
        rA = run_bass_kernel_spmd(ncA, mapsA, core_ids=cores).results
        lamb = np.stack([_bc(np.asarray(inp[k][l])) for k in ("lam_q1", "lam_k1", "lam_q2", "lam_k2")], axis=1)
        mapsB = []
        for c in cores:
            if c % 2 == 1:
                kTp = np.ascontiguousarray(np.asarray(rA[c - 1]["kT"])[:, :, 128:])
                vp = np.ascontiguousarray(np.asarray(rA[c - 1]["v"])[128:])
            else:
                kTp = np.zeros((NH, 128, SO), NPBF)
                vp = np.zeros((SO, 1024), NPBF)
            mapsB.append(dict(qT=rA[c]["qT"], kT=rA[c]["kT"], v=rA[c]["v"], kTp=kTp, vp=vp,
                              lamb=np.ascontiguousarray(lamb), ghead=_bc(np.asarray(inp["g_head"][l])),
                              alibi=alibi, mask=mask, ident=ident, flag=flags[c]))
        rB = run_bass_kernel_spmd(build_B(NT, l, acols, alibi.shape[1]), mapsB, core_ids=cores).results
        cwt = np.stack([_pk(inp["conv_w"][l][0], 2 * FC), _pk(inp["conv_w"][l][1], 2 * FC),
                        _pk(inp["conv_w"][l][2], 2 * FC), _pk(inp["conv_b"][l], 2 * FC)], axis=1)
        mapsC = [dict(x=xs[c], cat=rB[c]["cat"], catp=rA[c]["catp"], w_out=wb[("w_out", l)],
                      w_up=wb[("w_up", l)], w_down=wb[("w_down", l)],
                      gffn=_pk(inp["g_ffn_pre"][l], KC), flag=flags[c],
                      gpost=_bc(np.asarray(inp["g_mix_post"][l])), gpost2=_bc(np.asarray(inp["g_ffn_post"][l])),
                      cw=np.ascontiguousarray(cwt), ident=ident) for c in cores]
        rC = run_bass_kernel_spmd(ncC, mapsC, core_ids=cores).results
        xs = [np.asarray(rC[c]["xo"]) for c in cores]
    out = np.empty_like(x)
    for c in cores:
        b, r = divmod(c, 2)
        out[b, r * SO:(r + 1) * SO] = xs[c][128:]
    return out


def kernel(**inputs):
    return run_fused(inputs, NT=8, depth=DEPTH, ncores=8)
```
